# Optimizing a Trainium2 kernel written in Bass

```python
import jax, jax.numpy as jnp
from jax import lax
import numpy as np

D_MODEL = 1024
BATCH = 2
SEQ = 8192
DEPTH = 2

N_MIXERS = 2
N_A = (DEPTH + 1) // 2
N_B = DEPTH // 2
D_FF = 2816
LRU_WIDTH = D_MODEL
LRU_HEADS = 16
LRU_HEAD_DIM = LRU_WIDTH // LRU_HEADS
LRU_C = 8.0
CONV_WIDTH = 4
SG_WIDTH = D_MODEL
SG_HEADS = 8
SG_HEAD_DIM = SG_WIDTH // SG_HEADS
CHUNK = 128
NORM_EPS = 1e-6

kernel_name = "hybrid_rglru_chunked_gmlp_macaron"


def rms_norm(x, g):
    xf = x.astype(jnp.float32)
    y = xf * lax.rsqrt(jnp.mean(xf * xf, axis=-1, keepdims=True) + NORM_EPS)
    return (y * g.astype(jnp.float32)).astype(x.dtype)


def swiglu(x, w_in, w_out):
    g, u = jnp.split(x @ w_in, 2, axis=-1)
    return (jax.nn.silu(g) * u) @ w_out


def causal_depthwise_conv(x, w, b):
    k_width = w.shape[0]
    s = x.shape[1]
    xp = jnp.pad(x, ((0, 0), (k_width - 1, 0), (0, 0)))
    out = b + xp[:, 0:s] * w[0]
    for k in range(1, k_width):
        out = out + xp[:, k:k + s] * w[k]
    return out


def block_diag_linear(x, w, b):
    bsz, s, _ = x.shape
    h, dh, _ = w.shape
    xh = x.reshape(bsz, s, h, dh)
    return (jnp.einsum('bshi,hij->bshj', xh, w) + b).reshape(bsz, s, h * dh)


def rg_lru(x, w_a, b_a, w_x, b_x, lam):
    f32 = jnp.float32
    xf = x.astype(f32)
    gate_a = jax.nn.sigmoid(block_diag_linear(xf, w_a.astype(f32), b_a.astype(f32)))
    gate_x = jax.nn.sigmoid(block_diag_linear(xf, w_x.astype(f32), b_x.astype(f32)))
    log_a = -LRU_C * gate_a * jax.nn.softplus(-lam.astype(f32))
    a = jnp.exp(log_a)
    mult = jnp.sqrt(-jnp.expm1(2.0 * log_a))
    bx = xf * gate_x * mult

    def combine(left, right):
        a_l, b_l = left
        a_r, b_r = right
        return a_l * a_r, a_r * b_l + b_r

    _, h = lax.associative_scan(combine, (a, bx), axis=1)
    return h.astype(x.dtype)


def recurrent_mixer(x, w_in, conv_w, conv_b, gate_a_w, gate_a_b, gate_x_w, gate_x_b, lam, w_out):
    y, r = jnp.split(x @ w_in, 2, axis=-1)
    y = jax.nn.gelu(y)
    r = causal_depthwise_conv(r, conv_w, conv_b)
    h = rg_lru(r, gate_a_w, gate_a_b, gate_x_w, gate_x_b, lam)
    return (y * h) @ w_out


def spatial_gating_mixer(x, w_in, v_norm, w_s, b_s, w_out):
    bsz, s, _ = x.shape
    u, v = jnp.split(jax.nn.gelu(x @ w_in), 2, axis=-1)
    v = rms_norm(v, v_norm)
    h, p, _ = w_s.shape
    c = v.shape[-1] // h
    vc = v.reshape(bsz, s // p, p, h, c)
    causal = jnp.tril(jnp.ones((p, p), dtype=bool))
    ws = jnp.where(causal[None], w_s, jnp.zeros_like(w_s))
    sg = jnp.einsum('hpq,bnqhc->bnphc', ws, vc) + b_s.T[:, :, None]
    sg = sg.reshape(bsz, s, h * c)
    return (u * sg) @ w_out


def setup_inputs(seed: int = 0) -> dict:
    key = jax.random.key(seed)
    ks = jax.random.split(key, 32)
    f32 = jnp.float32

    def nrm(k, shape, fan_in):
        return jax.random.normal(k, shape, f32) * (fan_in ** -0.5)

    def gain(k, shape):
        return 1.0 + 0.05 * jax.random.normal(k, shape, f32)

    def small(k, shape):
        return 0.01 * jax.random.normal(k, shape, f32)

    x = jax.random.normal(ks[0], (BATCH, SEQ, D_MODEL), f32)

    a0 = jax.random.uniform(ks[14], (N_A, LRU_WIDTH), f32, 0.9, 0.999)
    p0 = a0 ** (1.0 / LRU_C)
    rec_lambda = jnp.log(p0) - jnp.log1p(-p0)

    return {
        "x": x,
        "ffn1_norm": gain(ks[1], (DEPTH, D_MODEL)),
        "ffn1_w_in": nrm(ks[2], (DEPTH, D_MODEL, 2 * D_FF), D_MODEL),
        "ffn1_w_out": nrm(ks[3], (DEPTH, D_FF, D_MODEL), D_FF),
        "mix_norm": gain(ks[4], (DEPTH, D_MODEL)),
        "ffn2_norm": gain(ks[5], (DEPTH, D_MODEL)),
        "ffn2_w_in": nrm(ks[6], (DEPTH, D_MODEL, 2 * D_FF), D_MODEL),
        "ffn2_w_out": nrm(ks[7], (DEPTH, D_FF, D_MODEL), D_FF),
        "rec_w_in": nrm(ks[8], (N_A, D_MODEL, 2 * LRU_WIDTH), D_MODEL),
        "rec_conv_w": nrm(ks[9], (N_A, CONV_WIDTH, LRU_WIDTH), CONV_WIDTH),
        "rec_conv_b": small(ks[10], (N_A, LRU_WIDTH)),
        "rec_gate_a_w": nrm(ks[11], (N_A, LRU_HEADS, LRU_HEAD_DIM, LRU_HEAD_DIM), LRU_HEAD_DIM),
        "rec_gate_a_b": small(ks[12], (N_A, LRU_HEADS, LRU_HEAD_DIM)),
        "rec_gate_x_w": nrm(ks[13], (N_A, LRU_HEADS, LRU_HEAD_DIM, LRU_HEAD_DIM), LRU_HEAD_DIM),
        "rec_gate_x_b": small(ks[15], (N_A, LRU_HEADS, LRU_HEAD_DIM)),
        "rec_lambda": rec_lambda,
        "rec_w_out": nrm(ks[16], (N_A, LRU_WIDTH, D_MODEL), LRU_WIDTH),
        "sg_w_in": nrm(ks[17], (N_B, D_MODEL, 2 * SG_WIDTH), D_MODEL),
        "sg_v_norm": gain(ks[18], (N_B, SG_WIDTH)),
        "sg_w_s": nrm(ks[19], (N_B, SG_HEADS, CHUNK, CHUNK), CHUNK),
        "sg_b_s": 1.0 + 0.1 * jax.random.normal(ks[20], (N_B, SG_HEADS, CHUNK), f32),
        "sg_w_out": nrm(ks[21], (N_B, SG_WIDTH, D_MODEL), SG_WIDTH),
        "final_norm": gain(ks[22], (D_MODEL,)),
    }


def reference(x, ffn1_norm, ffn1_w_in, ffn1_w_out, mix_norm, ffn2_norm, ffn2_w_in, ffn2_w_out,
              rec_w_in, rec_conv_w, rec_conv_b, rec_gate_a_w, rec_gate_a_b, rec_gate_x_w, rec_gate_x_b,
              rec_lambda, rec_w_out, sg_w_in, sg_v_norm, sg_w_s, sg_b_s, sg_w_out, final_norm):
    h = x
    for i in range(DEPTH):
        h = h + 0.5 * swiglu(rms_norm(h, ffn1_norm[i]), ffn1_w_in[i], ffn1_w_out[i])
        hn = rms_norm(h, mix_norm[i])
        j = i // N_MIXERS
        if i % N_MIXERS == 0:
            m = recurrent_mixer(hn, rec_w_in[j], rec_conv_w[j], rec_conv_b[j], rec_gate_a_w[j], rec_gate_a_b[j],
                                rec_gate_x_w[j], rec_gate_x_b[j], rec_lambda[j], rec_w_out[j])
        else:
            m = spatial_gating_mixer(hn, sg_w_in[j], sg_v_norm[j], sg_w_s[j], sg_b_s[j], sg_w_out[j])
        h = h + m
        h = h + 0.5 * swiglu(rms_norm(h, ffn2_norm[i]), ffn2_w_in[i], ffn2_w_out[i])
    return rms_norm(h, final_norm)
```

```python
import os
import numpy as np
import concourse.bass as bass
import concourse.mybir as mybir
from concourse.bass_utils import run_bass_kernel_spmd

F32 = mybir.dt.float32
BF16 = mybir.dt.bfloat16
AF = mybir.ActivationFunctionType
ALU = mybir.AluOpType
AX = mybir.AxisListType

D = 1024
DFF = 2816
NK = 8
NJ = 22
NTOK = 2048
HALO = 4
NCOL = NTOK + HALO
GT = 1024
NSLOT = 3
SLOT_ELEMS = 4096
EPS = 1e-6
NCORES = 8
WSTREAM_USED = 128 * (4 * (11 * 4096 + 8 * NJ * 128) + 8 * 2048 + 4 * 4096 + 2 * 4096 + 8 * 1024)
WSTREAM_TOTAL = WSTREAM_USED + 128

CV = {}
_names = ["ffn1_norm0", "ffn1_norm1", "mix_norm0", "mix_norm1", "ffn2_norm0", "ffn2_norm1", "final_norm",
          "sg_v_norm", "conv_b", "conv_w0", "conv_w1", "conv_w2", "conv_w3", "gate_a_b", "gate_x_b", "lam"]
for _i, _n in enumerate(_names):
    CV[_n] = _i * 8
NCV = len(_names) * 8


class Buf:
    __slots__ = ("w", "r", "name")

    def __init__(self, name=""):
        self.w = None
        self.r = []
        self.name = name


class Prog:
    def __init__(self, nc):
        self.nc = nc
        self.eng = {"pe": nc.tensor, "act": nc.scalar, "dve": nc.vector, "pool": nc.gpsimd, "sp": nc.sync}
        self.sem = {}
        self.cnt = {}
        self.seen = {c: {} for c in self.eng}
        self.pe_pending = []
        for e in ("pe", "act", "dve", "pool"):
            self.newsem(e)

    def newsem(self, name):
        self.sem[name] = self.nc.alloc_semaphore("s_" + name)
        self.cnt[name] = 0

    def wait(self, c, deps):
        need = {}
        for d in deps:
            if d is None:
                continue
            e, v = d
            if e == c and c == "pe":
                continue
            if v > self.seen[c].get(e, 0) and v > need.get(e, 0):
                need[e] = v
        for e, v in need.items():
            self.eng[c].wait_ge(self.sem[e], v)
            self.seen[c][e] = v

    def _deps(self, reads, writes):
        deps = []
        for b in reads:
            deps.append(b.w)
        for b in writes:
            deps.append(b.w)
            deps.extend(b.r)
        return deps

    def op(self, c, fn, reads=(), writes=()):
        self.wait(c, self._deps(reads, writes))
        ins = fn(self.eng[c])
        self.cnt[c] += 1
        v = self.cnt[c]
        ins.then_inc(self.sem[c], 1)
        for b in writes:
            b.w = (c, v)
            b.r = []
        for b in reads:
            b.r.append((c, v))
        return ins

    def mm(self, out_buf, out_ap, lhsT, rhs, reads, start, stop, hw=None):
        deps = [b.w for b in reads]
        if start:
            deps.append(out_buf.w)
            deps.extend(out_buf.r)
        self.wait("pe", deps)
        hs, he = (start, stop) if hw is None else hw
        ins = self.nc.tensor.matmul(out_ap, lhsT, rhs, start=hs, stop=he)
        self.pe_pending.extend(reads)
        if stop:
            self.cnt["pe"] += 1
            v = self.cnt["pe"]
            ins.then_inc(self.sem["pe"], 1)
            out_buf.w = ("pe", v)
            out_buf.r = []
            seen = set()
            for b in self.pe_pending:
                if id(b) not in seen:
                    seen.add(id(b))
                    b.r.append(("pe", v))
            self.pe_pending = []
        return ins

    def dma(self, issuer, q, out, in_, reads=(), writes=()):
        if q not in self.sem:
            self.newsem(q)
        self.wait(issuer, self._deps(reads, writes))
        ins = self.eng[issuer].dma_start(out=out, in_=in_)
        self.cnt[q] += 16
        v = self.cnt[q]
        ins.then_inc(self.sem[q], 16)
        for b in writes:
            b.w = (q, v)
            b.r = []
        for b in reads:
            b.r.append((q, v))
        return ins


def build_program(dbg_stage=99):
    nc = bass.Bass("TRN2", target_bir_lowering=False)
    P = Prog(nc)

    def din(name, shape):
        return nc.dram_tensor(name, list(shape), F32, kind="ExternalInput").ap()

    xT = din("xT", [D, NCOL])
    cvec_d = din("cvec", [128, NCV])
    cmask_d = din("cmask", [128, 8])
    wsT_d = din("wsT", [128, 8 * 128])
    biasbc_d = din("biasbc", [128, 8 * 128])
    gate_w = {0: din("rec_gate_a_w", [1, 16, 64, 64]), 1: din("rec_gate_x_w", [1, 16, 64, 64])}
    wstream = nc.dram_tensor("wstream", [WSTREAM_TOTAL], F32, kind="ExternalInput").ap()
    WSPEC = {}
    wtot = [0]
    yT = nc.dram_tensor("yT", [D, NTOK], F32, kind="ExternalOutput").ap()
    ag_in = nc.dram_tensor("ag_in", [128, 16], F32)
    ag_out = nc.dram_tensor("ag_out", [NCORES * 128, 16], F32)

    def sb(name, shape, dt):
        return nc.alloc_sbuf_tensor(name, list(shape), dt)

    X = sb("X", [128, NK, NCOL], F32)
    XN = sb("XN", [128, NK, NCOL], BF16)
    HIDB = sb("HIDB", [128, NJ * (GT + HALO)], BF16)
    RING = [sb(f"ring{i}", [128, SLOT_ELEMS], BF16) for i in range(NSLOT)]
    SGR = sb("SGR", [128, 2, 512], F32)
    RSTDR = sb("RSTDR", [128, 2, 512], F32)
    SG = [SGR[:, i, :] for i in range(2)]
    RSTD = [RSTDR[:, i, :] for i in range(2)]
    MX3 = sb("MX3", [128, GT], F32)
    CVEC = sb("CVEC", [128, NCV], F32)
    CMASK = sb("CMASK", [128, 8], F32)
    ONES = sb("ONES", [128, 128], BF16)
    GW = sb("GW", [128, NK, 2, 128], BF16)
    WST = sb("WST", [128, 8, 128], BF16)
    BIASBC = sb("BIASBC", [128, 8, 128], F32)
    SM = sb("SM", [128, 256], F32)
    PS = nc.alloc_psum_tensor("PS", [128, 8, 512], F32)

    c_eps = SM[:, 0:1]
    c_one = SM[:, 1:2]
    CNEG = SM[:, 8:16]
    CNEG2 = SM[:, 16:24]
    HST1 = SM[:, 24:32]
    HST2 = SM[:, 32:40]
    RTAIL = SM[:, 40:64]
    SGASUM = SM[:, 64:96]
    ABSEND = SM[:, 96:112]
    TMP8 = [SM[:, 112:120], SM[:, 120:128], SM[:, 128:136]]
    CNEGH = SM[:, 168:176]
    HGB = SM[:, 176:192]
    SSQ = SM[:, 136:152]
    SS1 = SM[:, 152:160]
    RS1 = SM[:, 160:168]
    AGSB = sb("AGSB", [128, NCORES, 16], F32)
    MX2 = sb("MX2", [128, 2 * (GT + 4) + 2 * GT + 2 * GT + GT], BF16)

    B = {}

    def buf(name):
        if name not in B:
            B[name] = Buf(name)
        return B[name]

    HID = HIDB[:, :].rearrange("p (j t) -> p j t", j=NJ)

    def ov_f32(off_elems_bf16, n_f32):
        return HIDB[:, off_elems_bf16:off_elems_bf16 + 2 * n_f32].bitcast(F32)

    def group_tiles(g, halo=False):
        ts = [(f"t{2 * g}", g * GT, 512, 0), (f"t{2 * g + 1}", g * GT + 512, 512, 512)]
        if halo:
            ts.append(("th", NTOK, HALO, GT))
        return ts

    bank_ctr = [0]

    def new_bank():
        b = bank_ctr[0] % 8
        bank_ctr[0] += 1
        return b, buf(f"bank{b}")

    unit_ctr = [0]

    PREF = {}

    def prefetch_unit(key, n, lo=0):
        r = load_unit(key, n, lo)
        PREF.setdefault((key, lo), []).append(r)

    def load_unit(key, n, lo=0, hi=None):
        hi = n if hi is None else hi
        if PREF.get((key, lo)):
            return PREF[(key, lo)].pop(0)
        if key not in WSPEC:
            WSPEC[key] = (wtot[0], n)
            wtot[0] += 128 * n
        off, n_ = WSPEC[key]
        assert n_ == n
        u = unit_ctr[0]
        unit_ctr[0] += 1
        s = u % NSLOT
        sbuf_ = buf(f"slot{s}")
        slot = RING[s]
        src = wstream[off:off + 128 * n].rearrange("(p n) -> p n", p=128)
        P.dma("pool", f"dw{s}", slot[:, lo:hi], src[:, lo:hi], reads=(), writes=(sbuf_,))
        return slot, sbuf_

    xbufs = lambda k, tn: buf(f"X{k}{tn}")
    xnbufs = lambda k, tn: buf(f"XN{k}{tn}")
    all_tiles = group_tiles(0, True) + group_tiles(1)

    P.dma("sp", "dc0", CVEC[:, :], cvec_d, writes=(buf("CVEC"),))
    P.dma("sp", "dc1", CMASK[:, :], cmask_d, writes=(buf("CMASK"),))
    P.dma("sp", "dc2", BIASBC[:, :, :], biasbc_d.rearrange("p (h q) -> p h q", h=8), writes=(buf("BIASBC"),))
    xTv = xT.rearrange("(k p) t -> p k t", p=128)
    for ti_, t_ in enumerate(all_tiles[:1] + all_tiles[2:3] + all_tiles[1:2] + all_tiles[3:]):
        tn_, c0_, w_, _g = t_
        P.dma("sp", f"dx{ti_}", X[:, :, c0_:c0_ + w_], xTv[:, :, c0_:c0_ + w_], writes=tuple(xbufs(k, tn_) for k in range(NK)))

    P.op("dve", lambda e: e.memset(ONES[:, :], 1.0), writes=(buf("ONES"),))
    P.op("dve", lambda e: e.memset(SM[:, 0:1], EPS), writes=(buf("SM_eps"),))
    P.op("dve", lambda e: e.memset(SM[:, 1:2], 1.0), writes=(buf("SM_one"),))
    P.op("dve", lambda e: e.memset(SM[:, 24:32], 0.0), writes=(buf("HST1"),))
    P.op("dve", lambda e: e.memset(SM[:, 64:96], 0.0), writes=tuple(buf(f"SGASUM{c}_{i}") for c in range(NK) for i in range(4)))
    P.op("dve", lambda e: e.memset(GW[:, :, :, :], 0.0), writes=(buf("GW"),))
    for gi in range(2):
        for half in range(2):
            src = gate_w[gi][0].rearrange("(k two) i j -> two i k j", two=2)[half]
            dst = GW[64 * half:64 * half + 64, :, gi, 64 * half:64 * half + 64]
            if gi == 0 and half == 0:
                P.dma("pool", "dgw", dst, src, writes=(buf("GW"),))
            else:
                ins = nc.gpsimd.dma_start(out=dst, in_=src)
                P.cnt["dgw"] += 16
                ins.then_inc(P.sem["dgw"], 16)
                buf("GW").w = ("dgw", P.cnt["dgw"])
    P.dma("pool", "dws", WST[:, :, :], wsT_d.rearrange("q (h p) -> q h p", h=8), writes=(buf("WST"),))
    P.op("pool", lambda e: e.affine_select(out=WST[:, :, :], in_=WST[:, :, :], pattern=[[0, 8], [1, 128]],
                                           compare_op=ALU.is_ge, fill=0.0, base=0, channel_multiplier=-1),
         reads=(buf("WST"),), writes=(buf("WST"),))
    lamc = CVEC[:, CV["lam"]:CV["lam"] + 8]
    P.op("act", lambda e: e.activation(out=CNEG, in_=lamc, func=AF.Exp, scale=-1.0), reads=(buf("CVEC"),), writes=(buf("CNEG"),))
    P.op("act", lambda e: e.activation(out=CNEG, in_=CNEG, func=AF.Ln, bias=c_one, scale=1.0),
         reads=(buf("CNEG"), buf("SM_one")), writes=(buf("CNEG"),))
    P.op("dve", lambda e: e.tensor_scalar(out=CNEG2, in0=CNEG, scalar1=-16.0, scalar2=None, op0=ALU.mult),
         reads=(buf("CNEG"),), writes=(buf("CNEG2"),))
    P.op("dve", lambda e: e.tensor_scalar(out=CNEGH, in0=CNEG, scalar1=-4.0, scalar2=None, op0=ALU.mult),
         reads=(buf("CNEG"),), writes=(buf("CNEGH"),))
    P.op("dve", lambda e: e.tensor_scalar(out=CNEG, in0=CNEG, scalar1=-8.0, scalar2=None, op0=ALU.mult),
         reads=(buf("CNEG"),), writes=(buf("CNEG"),))
    P.op("dve", lambda e: e.tensor_scalar(out=HGB, in0=CVEC[:, CV["gate_a_b"]:CV["gate_a_b"] + 16], scalar1=0.5, scalar2=None, op0=ALU.mult),
         reads=(buf("CVEC"),), writes=(buf("HGB"),))

    def square_tile(k, t):
        tn, c0, w, _ = t
        P.op("act", lambda e: e.activation(out=XN[:, k, c0:c0 + w], in_=X[:, k, c0:c0 + w], func=AF.Square),
             reads=(xbufs(k, tn),), writes=(xnbufs(k, tn),))

    rstd_ctr = [0]

    def norm_finish(t, gain_name, final=False):
        tn, c0, w, _ = t
        b, bb = new_bank()
        for k in range(NK):
            P.mm(bb, PS[:, b, 0:w], ONES[:, :], XN[:, k, c0:c0 + w], reads=(xnbufs(k, tn), buf("ONES")),
                 start=(k == 0), stop=(k == NK - 1))
        ri = rstd_ctr[0] % 2
        rstd_ctr[0] += 1
        rb = buf(f"rstd{ri}")
        R = RSTD[ri]
        P.op("act", lambda e: e.activation(out=R[:, 0:w], in_=PS[:, b, 0:w], func=AF.Ln, bias=c_eps, scale=1.0 / D),
             reads=(bb, buf("SM_eps")), writes=(rb,))
        P.op("act", lambda e: e.activation(out=R[:, 0:w], in_=R[:, 0:w], func=AF.Exp, scale=-0.5),
             reads=(rb,), writes=(rb,))
        gc = CV[gain_name]
        for k in range(NK):
            if final:
                P.op("dve", lambda e: e.scalar_tensor_tensor(out=X[:, k, c0:c0 + w], in0=X[:, k, c0:c0 + w],
                                                             scalar=CVEC[:, gc + k:gc + k + 1], in1=R[:, 0:w],
                                                             op0=ALU.mult, op1=ALU.mult),
                     reads=(rb, buf("CVEC")), writes=(xbufs(k, tn),))
            else:
                P.op("dve", lambda e: e.scalar_tensor_tensor(out=XN[:, k, c0:c0 + w], in0=X[:, k, c0:c0 + w],
                                                             scalar=CVEC[:, gc + k:gc + k + 1], in1=R[:, 0:w],
                                                             op0=ALU.mult, op1=ALU.mult),
                     reads=(xbufs(k, tn), rb, buf("CVEC")), writes=(xnbufs(k, tn),))

    sg_ctr = [0]

    def ffn_stage(layer, which, g, halo, next_gain, final=False):
        tiles = group_tiles(g, halo)
        for jp in range(NJ // 2):
            def vg(slot):
                return slot[:, 0:2048].rearrange("p (k c) -> p k c", k=NK)

            def vu(slot):
                return slot[:, 2048:4096].rearrange("p (k c) -> p k c", k=NK)
            slot, sbuf_ = load_unit(("ffn_in", which, layer, jp), 4096)
            wg = vg(slot)
            wu = vu(slot)
            for jj in range(2):
                j = 2 * jp + jj
                for t in tiles:
                    tn, c0, w, gc0 = t
                    bg, bgb = new_bank()
                    for k in range(NK):
                        P.mm(bgb, PS[:, bg, 0:w], wg[:, k, jj * 128:(jj + 1) * 128], XN[:, k, c0:c0 + w],
                             reads=(sbuf_, xnbufs(k, tn)), start=(k == 0), stop=(k == NK - 1))
                    bu, bub = new_bank()
                    for k in range(NK):
                        P.mm(bub, PS[:, bu, 0:w], wu[:, k, jj * 128:(jj + 1) * 128], XN[:, k, c0:c0 + w],
                             reads=(sbuf_, xnbufs(k, tn)), start=(k == 0), stop=(k == NK - 1))
                    si = sg_ctr[0] % 2
                    sg_ctr[0] += 1
                    sgb = buf(f"sg{si}")
                    S = SG[si]
                    P.op("act", lambda e: e.activation(out=S[:, 0:w], in_=PS[:, bg, 0:w], func=AF.Silu),
                         reads=(bgb,), writes=(sgb,))
                    hb = buf(f"HID{j}{tn}")
                    P.op("dve", lambda e: e.tensor_tensor(out=HID[:, j, gc0:gc0 + w], in0=S[:, 0:w], in1=PS[:, bu, 0:w],
                                                          op=ALU.mult),
                         reads=(sgb, bub), writes=(hb,))
        for m in range(NK):
            def vo(slot):
                return slot[:, 0:NJ * 128].rearrange("p (j c) -> p j c", j=NJ)
            slot, sbuf_ = load_unit(("ffn_out", which, layer, m), NJ * 128)
            wo = vo(slot)
            for t in tiles:
                tn, c0, w, gc0 = t
                b, bb = new_bank()
                for j in range(NJ):
                    P.mm(bb, PS[:, b, 0:w], wo[:, j, :], HID[:, j, gc0:gc0 + w],
                         reads=(sbuf_, buf(f"HID{j}{tn}")), start=(j == 0), stop=(j == NJ - 1))
                P.op("dve", lambda e: e.scalar_tensor_tensor(out=X[:, m, c0:c0 + w], in0=PS[:, b, 0:w], scalar=0.5,
                                                             in1=X[:, m, c0:c0 + w], op0=ALU.mult, op1=ALU.add),
                     reads=(bb,), writes=(xbufs(m, tn),))
                square_tile(m, t)
        for t in tiles:
            norm_finish(t, next_gain, final=final)

    o = 0
    YG0 = ov_f32(o, GT); o += 2 * GT
    RL0 = ov_f32(o, GT + 4); o += 2 * (GT + 4)
    RC0 = ov_f32(o, GT); o += 2 * GT
    SGA = ov_f32(o, GT); o += 2 * GT
    SGX = ov_f32(o, GT); o += 2 * GT
    A2 = ov_f32(o, GT); o += 2 * GT
    RCB0 = HIDB[:, o:o + GT]; o += GT
    YH = HIDB[:, o:o + NK * GT].rearrange("p (k t) -> p k t", k=NK); o += NK * GT
    assert o <= NJ * (GT + HALO), o
    o2 = 0
    RL1 = MX2[:, o2:o2 + 2 * (GT + 4)].bitcast(F32); o2 += 2 * (GT + 4)
    RC1 = MX2[:, o2:o2 + 2 * GT].bitcast(F32); o2 += 2 * GT
    YG1 = MX2[:, o2:o2 + 2 * GT].bitcast(F32); o2 += 2 * GT
    RCB1 = MX2[:, o2:o2 + GT]; o2 += GT
    RLs, RCs, YGs, RCBs = [RL0, RL1], [RC0, RC1], [YG0, YG1], [RCB0, RCB1]
    SGAs = [SGA, MX3[:, :]]
    SGXs = [SGX, RSTDR[:, :, :].rearrange("p a b -> p (a b)")]
    A2s = [A2, SGR[:, :, :].rearrange("p a b -> p (a b)")]

    def al_sgx(s_):
        return (buf("rstd0"), buf("rstd1")) if s_ == 1 else ()

    def al_a2(s_):
        return (buf("sg0"), buf("sg1")) if s_ == 1 else ()

    cb = CV["conv_b"]
    cw = [CV[f"conv_w{i}"] for i in range(4)]
    gab = CV["gate_a_b"]
    gxb = CV["gate_x_b"]

    def out_proj_stage(w_out_d, g, SRC, srcname, next_gain):
        tiles = group_tiles(g)
        for mg in range(2):
            def vo(slot):
                return slot[:, 0:4096].rearrange("p (k c) -> p k c", k=NK)
            slot, sbuf_ = load_unit(("oproj", w_out_d, mg), 4096)
            wo = vo(slot)
            for t in tiles:
                tn, c0, w, gc0 = t
                for mm_ in range(4):
                    m = mg * 4 + mm_
                    b, bb = new_bank()
                    for k in range(NK):
                        P.mm(bb, PS[:, b, 0:w], wo[:, k, mm_ * 128:(mm_ + 1) * 128], SRC[:, k, gc0:gc0 + w],
                             reads=(sbuf_, buf(f"{srcname}{k}")), start=(k == 0), stop=(k == NK - 1))
                    P.op("dve", lambda e: e.tensor_tensor(out=X[:, m, c0:c0 + w], in0=PS[:, b, 0:w],
                                                          in1=X[:, m, c0:c0 + w], op=ALU.add),
                         reads=(bb,), writes=(xbufs(m, tn),))
                    square_tile(m, t)
        for t in tiles:
            norm_finish(t, next_gain)

    def rglru_stage(pas, g, next_gain=None):
        tiles = group_tiles(g)
        HST = HST1 if pas == 1 else HST2
        hstb = buf("HST1" if pas == 1 else "HST2")
        ctx = {}

        def F1(c):
            s_ = c % 2
            slot, sbuf_ = load_unit(("rec_in", c), 2048, lo=(1024 if pas == 1 else 0))
            wy = slot[:, 0:1024].rearrange("p (k c) -> p k c", k=NK)
            wr = slot[:, 1024:2048].rearrange("p (k c) -> p k c", k=NK)
            d = {"r": [], "y": [], "h": None}
            if g == 0:
                b, bb = new_bank()
                for k in range(NK):
                    P.mm(bb, PS[:, b, 0:HALO], wr[:, k, :], XN[:, k, NTOK:NTOK + HALO],
                         reads=(sbuf_, xnbufs(k, "th")), start=(k == 0), stop=(k == NK - 1))
                d["h"] = (b, bb)
            for ti, t in enumerate(tiles):
                tn, c0, w, gc0 = t
                b, bb = new_bank()
                for k in range(NK):
                    P.mm(bb, PS[:, b, 0:w], wr[:, k, :], XN[:, k, c0:c0 + w],
                         reads=(sbuf_, xnbufs(k, tn)), start=(k == 0), stop=(k == NK - 1))
                d["r"].append((b, bb))
                if pas == 2:
                    b2, bb2 = new_bank()
                    for k in range(NK):
                        P.mm(bb2, PS[:, b2, 0:w], wy[:, k, :], XN[:, k, c0:c0 + w],
                             reads=(sbuf_, xnbufs(k, tn)), start=(k == 0), stop=(k == NK - 1))
                    d["y"].append((b2, bb2))
            ctx[c] = d

        def F2(c):
            s_ = c % 2
            RL, YG = RLs[s_], YGs[s_]
            d = ctx[c]
            if g == 0:
                b, bb = d["h"]
                P.op("act", lambda e: e.activation(out=RL[:, 0:3], in_=PS[:, b, 1:4], func=AF.Identity),
                     reads=(bb,), writes=(buf(f"RLh{s_}"),))
            else:
                P.op("act", lambda e: e.activation(out=RL[:, 0:3], in_=RTAIL[:, 3 * c:3 * c + 3], func=AF.Identity),
                     reads=(buf(f"RTAIL{c}"),), writes=(buf(f"RLh{s_}"),))
            for ti, t in enumerate(tiles):
                tn, c0, w, gc0 = t
                b, bb = d["r"][ti]
                P.op("act", lambda e: e.activation(out=RL[:, 3 + gc0:3 + gc0 + w], in_=PS[:, b, 0:w], func=AF.Identity),
                     reads=(bb,), writes=(buf(f"RL{ti}{s_}"),))
            if pas == 2:
                for ti, t in enumerate(tiles):
                    tn, c0, w, gc0 = t
                    b2, bb2 = d["y"][ti]
                    P.op("act", lambda e: e.activation(out=YG[:, gc0:gc0 + w], in_=PS[:, b2, 0:w], func=AF.Gelu_apprx_tanh),
                         reads=(bb2,), writes=(buf(f"YG{ti}{s_}"),))

        def F3(c):
            s_ = c % 2
            RL, RC, RCB = RLs[s_], RCs[s_], RCBs[s_]
            rl_all = (buf(f"RLh{s_}"), buf(f"RL0{s_}"), buf(f"RL1{s_}"))
            rcb_ = buf(f"RC{s_}")
            if g == 0:
                P.op("dve", lambda e: e.tensor_copy(out=RTAIL[:, 3 * c:3 * c + 3], in_=RL[:, GT:GT + 3]),
                     reads=(buf(f"RL1{s_}"),), writes=(buf(f"RTAIL{c}"),))
            P.op("dve", lambda e: e.tensor_scalar(out=RC[:, :], in0=RL[:, 0:GT], scalar1=CVEC[:, cw[0] + c:cw[0] + c + 1],
                                                  scalar2=CVEC[:, cb + c:cb + c + 1], op0=ALU.mult, op1=ALU.add),
                 reads=rl_all + (buf("CVEC"),), writes=(rcb_,))
            for kk in range(1, 4):
                P.op("dve", lambda e: e.scalar_tensor_tensor(out=RC[:, :], in0=RL[:, kk:kk + GT],
                                                             scalar=CVEC[:, cw[kk] + c:cw[kk] + c + 1], in1=RC[:, :],
                                                             op0=ALU.mult, op1=ALU.add),
                     reads=rl_all + (rcb_,), writes=(rcb_,))
            P.op("dve", lambda e: e.tensor_copy(out=RCB[:, :], in_=RC[:, :]),
                 reads=(rcb_,), writes=(buf(f"RCB{s_}"),))

        def F4(c):
            s_ = c % 2
            RCB = RCBs[s_]
            d = ctx[c]
            d["g"] = []
            for ti, t in enumerate(tiles):
                tn, c0, w, gc0 = t
                ba, bab = new_bank()
                P.mm(bab, PS[:, ba, 0:w], GW[:, c, 0, :], RCB[:, gc0:gc0 + w], reads=(buf("GW"), buf(f"RCB{s_}")),
                     start=True, stop=True)
                bx_, bxb_ = new_bank()
                P.mm(bxb_, PS[:, bx_, 0:w], GW[:, c, 1, :], RCB[:, gc0:gc0 + w], reads=(buf("GW"), buf(f"RCB{s_}")),
                     start=True, stop=True)
                d["g"].append((ba, bab, bx_, bxb_))

        def B1(c):
            d = ctx[c]
            s_ = c % 2
            SGA_, SGX_, A2_ = SGAs[s_], SGXs[s_], A2s[s_]
            for ti, t in enumerate(tiles):
                tn, c0, w, gc0 = t
                ba, bab, bx_, bxb_ = d["g"][ti]
                tix = 2 * g + ti
                if pas == 1:
                    P.op("act", lambda e: e.activation(out=SGA_[:, gc0:gc0 + w], in_=PS[:, ba, 0:w], func=AF.Tanh,
                                                       bias=HGB[:, c:c + 1], scale=0.5,
                                                       accum_out=SGASUM[:, 4 * c + tix:4 * c + tix + 1]),
                         reads=(bab, buf("HGB")), writes=(buf(f"SGA{ti}_{s_}"), buf(f"SGASUM{c}_{tix}")))
                else:
                    P.op("act", lambda e: e.activation(out=SGA_[:, gc0:gc0 + w], in_=PS[:, ba, 0:w], func=AF.Tanh,
                                                       bias=HGB[:, c:c + 1], scale=0.5),
                         reads=(bab, buf("HGB")), writes=(buf(f"SGA{ti}_{s_}"),))
                P.op("act", lambda e: e.activation(out=SGX_[:, gc0:gc0 + w], in_=PS[:, bx_, 0:w], func=AF.Tanh,
                                                   bias=HGB[:, 8 + c:8 + c + 1], scale=0.5),
                     reads=(bxb_, buf("HGB")), writes=(buf(f"SGX{ti}_{s_}"),) + al_sgx(s_))
            sga_all = (buf(f"SGA0_{s_}"), buf(f"SGA1_{s_}"))
            a2b = (buf(f"A2_{s_}"),) + al_a2(s_)
            P.op("act", lambda e: e.activation(out=A2_, in_=SGA_, func=AF.Exp, scale=CNEG[:, c:c + 1], bias=CNEG[:, c:c + 1]),
                 reads=sga_all + (buf("CNEG"),), writes=a2b)
            P.op("act", lambda e: e.activation(out=SGA_, in_=SGA_, func=AF.Exp, scale=CNEGH[:, c:c + 1], bias=CNEGH[:, c:c + 1]),
                 reads=sga_all + (buf("CNEGH"),), writes=sga_all)
            P.op("act", lambda e: e.activation(out=A2_, in_=A2_, func=AF.Sqrt, bias=c_one, scale=-1.0),
                 reads=a2b + (buf("SM_one"),), writes=a2b)

        def B2(c):
            s_ = c % 2
            RC, YG = RCs[s_], YGs[s_]
            SGA_, SGX_, A2_ = SGAs[s_], SGXs[s_], A2s[s_]
            rcb_ = buf(f"RC{s_}")
            sga_all = (buf(f"SGA0_{s_}"), buf(f"SGA1_{s_}"))
            sgx_all = (buf(f"SGX0_{s_}"), buf(f"SGX1_{s_}")) + al_sgx(s_)
            a2b = (buf(f"A2_{s_}"),) + al_a2(s_)
            if pas == 2 and g == 0 and c == 0:
                exchange_fold()
            P.op("dve", lambda e: e.scalar_tensor_tensor(out=SGX_, in0=SGX_, scalar=1.0, in1=RC[:, :],
                                                         op0=ALU.add, op1=ALU.mult),
                 reads=sgx_all + (rcb_,), writes=sgx_all)
            P.op("dve", lambda e: e.scalar_tensor_tensor(out=SGX_, in0=SGX_, scalar=0.5, in1=A2_,
                                                         op0=ALU.mult, op1=ALU.mult),
                 reads=sgx_all + a2b, writes=sgx_all)
            P.op("dve", lambda e: e.tensor_tensor_scan(out=RC[:, :], data0=SGA_, data1=SGX_,
                                                       initial=HST[:, c:c + 1], op0=ALU.mult, op1=ALU.add),
                 reads=sga_all + sgx_all + (hstb,), writes=(rcb_,))
            P.op("dve", lambda e: e.tensor_copy(out=HST[:, c:c + 1], in_=RC[:, GT - 1:GT]),
                 reads=(rcb_,), writes=(hstb,))
            if pas == 2:
                P.op("dve", lambda e: e.tensor_tensor(out=YH[:, c, :], in0=YG[:, :], in1=RC[:, :], op=ALU.mult),
                     reads=(buf(f"YG0{s_}"), buf(f"YG1{s_}"), rcb_), writes=(buf(f"YH{c}"),))

        F1(0)
        F2(0)
        F3(0)
        for c in range(NK):
            if c + 1 < NK:
                F1(c + 1)
                F2(c + 1)
            F4(c)
            B1(c)
            if c + 1 < NK:
                F3(c + 1)
            B2(c)
        if pas == 2:
            out_proj_stage("rec_w_out", g, YH, "YH", next_gain)

    def exchange():
        for c in range(NK):
            P.op("dve", lambda e: e.tensor_reduce(out=TMP8[0][:, c:c + 1], in_=SGASUM[:, 4 * c:4 * c + 4], axis=AX.X, op=ALU.add),
                 reads=tuple(buf(f"SGASUM{c}_{i}") for i in range(4)), writes=(buf("TMP8_0"),))
        P.op("dve", lambda e: e.scalar_tensor_tensor(out=TMP8[0], in0=TMP8[0], scalar=float(NTOK), in1=CNEGH,
                                                     op0=ALU.add, op1=ALU.mult),
             reads=(buf("TMP8_0"), buf("CNEGH")), writes=(buf("TMP8_0"),))
        P.op("act", lambda e: e.activation(out=ABSEND[:, 0:8], in_=TMP8[0], func=AF.Exp),
             reads=(buf("TMP8_0"),), writes=(buf("ABSEND_A"),))
        P.op("dve", lambda e: e.tensor_copy(out=ABSEND[:, 8:16], in_=HST1), reads=(buf("HST1"),), writes=(buf("ABSEND_B"),))
        P.dma("sp", "dag0", ag_in[:, :], ABSEND, reads=(buf("ABSEND_A"), buf("ABSEND_B")), writes=(buf("ag_in"),))
        for c_ in range(NSLOT):
            prefetch_unit(("rec_in", c_), 2048, 0)
        P.wait("pool", [r_[1].w for r_ in (PREF[(("rec_in", c_), 0)][0] for c_ in range(NSLOT))])
        P.wait("pool", [buf("ag_in").w])
        P.newsem("cc")
        ins = nc.gpsimd.collective_compute("AllGather", ALU.bypass, replica_groups=[list(range(NCORES))],
                                           ins=[ag_in.ap().opt()], outs=[ag_out.ap().opt()])
        ins.then_inc(P.sem["cc"], 1)
        P.cnt["cc"] = 1
        buf("ag_out").w = ("cc", 1)
        P.wait("pool", [("cc", 1)])
        P.dma("sp", "dag1", AGSB[:, :, :], ag_out.ap().rearrange("(r p) f -> p r f", p=128),
              reads=(buf("ag_out"),), writes=(buf("AGSB"),))

    def exchange_fold():
        P.op("dve", lambda e: e.memset(HST2, 0.0), writes=(buf("HST2"),))
        for r in range(NCORES - 1):
            mr = CMASK[:, r:r + 1]
            P.op("dve", lambda e: e.tensor_scalar(out=TMP8[1], in0=AGSB[:, r, 0:8], scalar1=-1.0, scalar2=mr,
                                                  op0=ALU.add, op1=ALU.mult),
                 reads=(buf("AGSB"), buf("CMASK")), writes=(buf("TMP8_1"),))
            P.op("dve", lambda e: e.scalar_tensor_tensor(out=TMP8[2], in0=TMP8[1], scalar=1.0, in1=HST2,
                                                         op0=ALU.add, op1=ALU.mult),
                 reads=(buf("TMP8_1"), buf("HST2")), writes=(buf("TMP8_2"),))
            P.op("dve", lambda e: e.scalar_tensor_tensor(out=HST2, in0=AGSB[:, r, 8:16], scalar=mr, in1=TMP8[2],
                                                         op0=ALU.mult, op1=ALU.add),
                 reads=(buf("AGSB"), buf("CMASK"), buf("TMP8_2")), writes=(buf("HST2"),))

    o = 0
    VN = HIDB[:, o:o + 8 * 1024].rearrange("p (n f) -> p n f", n=8); o += 8 * 1024
    UH = HIDB[:, o:o + NK * GT].rearrange("p (k t) -> p k t", k=NK); o += NK * GT
    UGs = [ov_f32(o, GT), ov_f32(o + 2 * GT, GT)]; o += 4 * GT
    assert o <= NJ * (GT + HALO), o
    WSN = [MX2[:, n * 1024:(n + 1) * 1024] for n in range(7)] + [MX3[:, :].bitcast(BF16)[:, 0:1024]]
    WSTf = WST[:, :, :].rearrange("p h q -> p (h q)")
    gvc = CV["sg_v_norm"]

    def sgu_stage(g, next_gain):
        tiles = group_tiles(g)

        def vfull(slot):
            return slot[:, 0:4096].rearrange("p (k c) -> p k c", k=NK)
        P.op("dve", lambda e: e.memset(SSQ, 0.0), writes=tuple(buf(f"SSQ{n}_{h}") for n in range(8) for h in range(2)))
        slots = []
        for half in range(2):
            slots.append(load_unit(("sg_v", half), 4096))
        for n in range(8):
            tn = tiles[n // 4][0]
            col0 = g * GT + n * 128
            for half in range(2):
                slot, sbuf_ = slots[half]
                wvv = vfull(slot)
                b, bb = new_bank()
                for k in range(NK):
                    P.mm(bb, PS[:, b, :], XN[:, k, col0:col0 + 128], wvv[:, k, :],
                         reads=(sbuf_, xnbufs(k, tn)), start=(k == 0), stop=(k == NK - 1))
                P.op("act", lambda e: e.activation(out=VN[:, n, half * 512:(half + 1) * 512], in_=PS[:, b, :], func=AF.Gelu_apprx_tanh),
                     reads=(bb,), writes=(buf(f"VN{n}_{half}"),))
                si = sg_ctr[0] % 2
                sg_ctr[0] += 1
                P.op("act", lambda e: e.activation(out=SG[si], in_=VN[:, n, half * 512:(half + 1) * 512], func=AF.Square,
                                                   accum_out=SSQ[:, 2 * n + half:2 * n + half + 1]),
                     reads=(buf(f"VN{n}_{half}"),), writes=(buf(f"sg{si}"), buf(f"SSQ{n}_{half}")))
        ssq_all = tuple(buf(f"SSQ{n}_{h}") for n in range(8) for h in range(2))
        SSQv = SSQ.rearrange("p (n h) -> p n h", h=2)
        P.op("dve", lambda e: e.tensor_tensor(out=SS1, in0=SSQv[:, :, 0], in1=SSQv[:, :, 1], op=ALU.add),
             reads=ssq_all, writes=(buf("SS1"),))
        P.op("act", lambda e: e.activation(out=RS1, in_=SS1, func=AF.Ln, bias=c_eps, scale=1.0 / D),
             reads=(buf("SS1"), buf("SM_eps")), writes=(buf("RS1"),))
        P.op("act", lambda e: e.activation(out=RS1, in_=RS1, func=AF.Exp, scale=-0.5),
             reads=(buf("RS1"),), writes=(buf("RS1"),))
        for n in range(8):
            P.op("dve", lambda e: e.tensor_scalar(out=WSN[n], in0=WSTf, scalar1=RS1[:, n:n + 1], scalar2=None, op0=ALU.mult),
                 reads=(buf("WST"), buf("RS1")), writes=(buf(f"WSN{n}"),))
        for c in range(NK):
            UG = UGs[c % 2]
            us = c % 2

            def vu(slot):
                return slot[:, 0:1024].rearrange("p (k c) -> p k c", k=NK)
            slot, sbuf_ = load_unit(("sg_u", c), 1024)
            wu = vu(slot)
            ub = []
            for ti, t in enumerate(tiles):
                tn, c0, w, gc0 = t
                b, bb = new_bank()
                for k in range(NK):
                    P.mm(bb, PS[:, b, 0:w], wu[:, k, :], XN[:, k, c0:c0 + w],
                         reads=(sbuf_, xnbufs(k, tn)), start=(k == 0), stop=(k == NK - 1))
                ub.append((b, bb))
            for ti, t in enumerate(tiles):
                tn, c0, w, gc0 = t
                b, bb = ub[ti]
                P.op("act", lambda e: e.activation(out=UG[:, gc0:gc0 + w], in_=PS[:, b, 0:w], func=AF.Gelu_apprx_tanh),
                     reads=(bb,), writes=(buf(f"UG{ti}_{us}"),))
            for ti, t in enumerate(tiles):
                tn, c0, w, gc0 = t
                bs, bsb = new_bank()
                for nn in range(4):
                    n = 4 * ti + nn
                    P.mm(bsb, PS[:, bs, nn * 128:(nn + 1) * 128], VN[:, n, c * 128:(c + 1) * 128], WSN[n][:, c * 128:(c + 1) * 128],
                         reads=(buf(f"VN{n}_0"), buf(f"VN{n}_1"), buf(f"WSN{n}")), start=(nn == 0), stop=(nn == 3), hw=(True, True))
                si = sg_ctr[0] % 2
                sg_ctr[0] += 1
                S = SG[si]
                for nn in range(4):
                    P.op("dve", lambda e: e.scalar_tensor_tensor(out=S[:, nn * 128:(nn + 1) * 128], in0=PS[:, bs, nn * 128:(nn + 1) * 128],
                                                                 scalar=CVEC[:, gvc + c:gvc + c + 1], in1=BIASBC[:, c, :],
                                                                 op0=ALU.mult, op1=ALU.add),
                         reads=(bsb, buf("CVEC"), buf("BIASBC")), writes=(buf(f"sg{si}"),))
                P.op("dve", lambda e: e.tensor_tensor(out=UH[:, c, gc0:gc0 + w], in0=S[:, 0:w], in1=UG[:, gc0:gc0 + w], op=ALU.mult),
                     reads=(buf(f"sg{si}"), buf(f"UG{ti}_{us}")), writes=(buf(f"UH{c}"),))
        out_proj_stage("sg_w_out", g, UH, "UH", next_gain)

    for t in all_tiles:
        for k in range(NK):
            square_tile(k, t)
        norm_finish(t, "ffn1_norm0")

    stage = 0

    def go(n):
        return dbg_stage >= n

    if go(1):
        ffn_stage(0, 1, 0, True, "mix_norm0")
        ffn_stage(0, 1, 1, False, "mix_norm0")
    if go(2):
        rglru_stage(1, 0)
        rglru_stage(1, 1)
        exchange()
    if go(3):
        rglru_stage(2, 0, "ffn2_norm0")
        rglru_stage(2, 1, "ffn2_norm0")
    if go(4):
        ffn_stage(0, 2, 0, False, "ffn1_norm1")
        ffn_stage(0, 2, 1, False, "ffn1_norm1")
    if go(5):
        ffn_stage(1, 1, 0, False, "mix_norm1")
        ffn_stage(1, 1, 1, False, "mix_norm1")
    if go(6):
        sgu_stage(0, "ffn2_norm1")
        sgu_stage(1, "ffn2_norm1")
    if go(7):
        ffn_stage(1, 2, 0, False, "final_norm", final=True)
        ffn_stage(1, 2, 1, False, "final_norm", final=True)

    yTv = yT.rearrange("(k p) t -> p k t", p=128)
    for g in range(2):
        tiles = group_tiles(g)
        rd = tuple(xbufs(k, t[0]) for k in range(NK) for t in tiles)
        P.dma("sp", "dout", yTv[:, :, g * GT:(g + 1) * GT], X[:, :, g * GT:(g + 1) * GT], reads=rd)
    nc.sync.wait_ge(P.sem["dout"], P.cnt["dout"])
    assert wtot[0] <= WSTREAM_TOTAL, wtot[0]
    return nc, WSPEC


_NC_CACHE = {}


def _pack_unit(key, inp):
    kind = key[0]
    if kind == "ffn_in":
        _, which, layer, jp = key
        W = np.asarray(inp[f"ffn{which}_w_in"][layer], dtype=np.float32).reshape(NK, 128, 2 * DFF)
        g = W[:, :, jp * 256:(jp + 1) * 256].transpose(1, 0, 2).reshape(128, 2048)
        u = W[:, :, DFF + jp * 256:DFF + (jp + 1) * 256].transpose(1, 0, 2).reshape(128, 2048)
        return np.concatenate([g, u], axis=1)
    if kind == "ffn_out":
        _, which, layer, m = key
        W = np.asarray(inp[f"ffn{which}_w_out"][layer], dtype=np.float32).reshape(NJ, 128, D)
        return W[:, :, m * 128:(m + 1) * 128].transpose(1, 0, 2).reshape(128, NJ * 128)
    if kind == "oproj":
        _, name, mg = key
        W = np.asarray(inp[name][0], dtype=np.float32).reshape(NK, 128, D)
        return W[:, :, mg * 512:(mg + 1) * 512].transpose(1, 0, 2).reshape(128, 4096)
    if kind == "rec_in":
        _, c = key
        W = np.asarray(inp["rec_w_in"][0], dtype=np.float32).reshape(NK, 128, 2 * D)
        y = W[:, :, c * 128:(c + 1) * 128].transpose(1, 0, 2).reshape(128, 1024)
        r = W[:, :, D + c * 128:D + (c + 1) * 128].transpose(1, 0, 2).reshape(128, 1024)
        return np.concatenate([y, r], axis=1)
    if kind == "sg_v":
        _, half = key
        W = np.asarray(inp["sg_w_in"][0], dtype=np.float32).reshape(NK, 128, 2 * D)
        return W[:, :, D + half * 512:D + (half + 1) * 512].transpose(1, 0, 2).reshape(128, 4096)
    if kind == "sg_u":
        _, c = key
        W = np.asarray(inp["sg_w_in"][0], dtype=np.float32).reshape(NK, 128, 2 * D)
        return W[:, :, c * 128:(c + 1) * 128].transpose(1, 0, 2).reshape(128, 1024)
    raise KeyError(key)


def _prep_inputs(inp, wspec):
    x = np.asarray(inp["x"], dtype=np.float32)

    def chunked(v):
        v = np.asarray(v, dtype=np.float32).reshape(NK, 128)
        return v.T

    cols = [chunked(inp["ffn1_norm"][0]), chunked(inp["ffn1_norm"][1]), chunked(inp["mix_norm"][0]),
            chunked(inp["mix_norm"][1]), chunked(inp["ffn2_norm"][0]), chunked(inp["ffn2_norm"][1]),
            chunked(inp["final_norm"]), chunked(inp["sg_v_norm"][0]), chunked(inp["rec_conv_b"][0])]
    for i in range(4):
        cols.append(chunked(inp["rec_conv_w"][0][i]))
    cols += [chunked(np.asarray(inp["rec_gate_a_b"][0]).reshape(-1)), chunked(np.asarray(inp["rec_gate_x_b"][0]).reshape(-1)),
             chunked(inp["rec_lambda"][0])]
    cvec = np.ascontiguousarray(np.concatenate(cols, axis=1), dtype=np.float32)
    wsT = np.ascontiguousarray(np.asarray(inp["sg_w_s"][0], dtype=np.float32).transpose(2, 0, 1).reshape(128, 8 * 128))
    biasbc = np.ascontiguousarray(np.broadcast_to(np.asarray(inp["sg_b_s"][0], dtype=np.float32).reshape(1, 8 * 128), (128, 8 * 128)))
    wstream = np.zeros((WSTREAM_TOTAL,), dtype=np.float32)
    for key, (off, n) in wspec.items():
        wstream[off:off + 128 * n] = _pack_unit(key, inp).reshape(-1)
    shared = {
        "cvec": cvec, "wsT": wsT, "biasbc": biasbc, "wstream": wstream,
        "rec_gate_a_w": np.asarray(inp["rec_gate_a_w"], dtype=np.float32), "rec_gate_x_w": np.asarray(inp["rec_gate_x_w"], dtype=np.float32),
    }
    in_maps = []
    for c in range(NCORES):
        b, q = c // 4, c % 4
        xt = np.zeros((D, NCOL), dtype=np.float32)
        xt[:, :NTOK] = x[b, q * NTOK:(q + 1) * NTOK, :].T
        if q > 0:
            xt[:, NTOK:] = x[b, q * NTOK - HALO:q * NTOK, :].T
        cm = np.zeros((128, 8), dtype=np.float32)
        for r in range(NCORES):
            if r // 4 == b and r % 4 < q:
                cm[:, r] = 1.0
        m = dict(shared)
        ws = wstream.copy()
        ws[WSTREAM_USED:] = float(c)
        m["wstream"] = ws
        m["xT"] = xt
        m["cmask"] = cm
        in_maps.append(m)
    return in_maps


def kernel(**inputs):
    dbg = int(os.environ.get("MK_DBG_STAGE", "99"))
    if dbg not in _NC_CACHE:
        _NC_CACHE[dbg] = build_program(dbg)
    nc, wspec = _NC_CACHE[dbg]
    in_maps = _prep_inputs(inputs, wspec)
    res = run_bass_kernel_spmd(nc, in_maps, core_ids=list(range(NCORES)))
    out = np.empty((2, 4 * NTOK, D), dtype=np.float32)
    for c in range(NCORES):
        b, q = c // 4, c % 4
        out[b, q * NTOK:(q + 1) * NTOK, :] = res.results[c]["yT"].T
    return out
```
